# Optimizing a Trainium2 kernel written in Bass

```python
import numpy as np
import jax
import jax.numpy as jnp
from jax import lax


D_MODEL = 2048
BATCH = 2
SEQ = 4096
DEPTH = 4
DEC_BATCH = 1
DEC_SEQ = 8192
PAST_LEN = 128

GRID_W = 64
NA_HEADS = 32
NA_HEAD_DIM = D_MODEL // NA_HEADS
NA_KH = 8
NA_KW = 16
NA_QCB = 16
NA_KB = NA_KW + NA_QCB
SSM_D_INNER = 2 * D_MODEL
SSM_HEAD_DIM = 64
SSM_HEADS = SSM_D_INNER // SSM_HEAD_DIM
SSM_GROUPS = 8
SSM_STATE = 128
SSM_CONV = 5
SSM_CHUNK = 128
SSM_GN = SSM_GROUPS * SSM_STATE
SSM_CONV_DIM = SSM_D_INNER + 2 * SSM_GN
SSM_IN_DIM = SSM_D_INNER + SSM_CONV_DIM + 2 * SSM_HEADS
MLP_HIDDEN = 4 * D_MODEL
N_MIXERS = 2
N_NA_LAYERS = (DEPTH + 1) // 2
N_SSM_LAYERS = DEPTH // 2
RMS_EPS = 1e-5

kernel_name = 'hybrid_natten_ssd_encoder'


def rms_norm(x, w):
    x32 = x.astype(jnp.float32)
    y = x32 * lax.rsqrt(jnp.mean(x32 * x32, axis=-1, keepdims=True) + RMS_EPS)
    return (y * w.astype(jnp.float32)).astype(x.dtype)


def neighbourhood_attention(h, qkv_w, qkv_b, rpb, out_w, out_b):
    b, t, _ = h.shape
    rows = t // GRID_W
    kh = min(NA_KH, rows)
    qkv = (h @ qkv_w + qkv_b).reshape(b, rows, GRID_W, 3, NA_HEADS, NA_HEAD_DIM)
    qkv = jnp.transpose(qkv, (3, 0, 4, 1, 2, 5))
    q = qkv[0] * (NA_HEAD_DIM ** -0.5)
    k, v = qkv[1], qkv[2]
    ncb = GRID_W // NA_QCB
    blk = np.arange(ncb)
    kb0 = np.clip(blk * NA_QCB - NA_KW // 2, 0, GRID_W - NA_KB)
    kcols = kb0[:, None] + np.arange(NA_KB)[None, :]
    qcols = blk[:, None] * NA_QCB + np.arange(NA_QCB)[None, :]
    cs = np.clip(qcols - NA_KW // 2, 0, GRID_W - NA_KW)
    kc = kcols[:, None, :]
    valid = (kc >= cs[..., None]) & (kc < cs[..., None] + NA_KW)
    dc = np.clip(kc - qcols[..., None], -(NA_KW - 1), NA_KW - 1) + (NA_KW - 1)
    col_bias = rpb[:, :, dc].astype(jnp.float32)
    valid_m = jnp.asarray(valid[:, :, None, :])

    def one_row(r):
        rs = jnp.clip(r - kh // 2, 0, rows - kh)
        qr = lax.dynamic_index_in_dim(q, r, axis=2, keepdims=False)
        qr = qr.reshape(b, NA_HEADS, ncb, NA_QCB, NA_HEAD_DIM)
        kr = lax.dynamic_slice_in_dim(k, rs, kh, axis=2)[:, :, :, kcols]
        vr = lax.dynamic_slice_in_dim(v, rs, kh, axis=2)[:, :, :, kcols]
        s = jnp.einsum('bhnqd,bhrnkd->bhnqrk', qr, kr).astype(jnp.float32)
        dr = rs + jnp.arange(kh) - r + (NA_KH - 1)
        bias = jnp.transpose(jnp.take(col_bias, dr, axis=1), (0, 2, 3, 1, 4))
        s = jnp.where(valid_m, s + bias[None], -1e30)
        p = jax.nn.softmax(s.reshape(s.shape[:4] + (kh * NA_KB,)), axis=-1).reshape(s.shape)
        o = jnp.einsum('bhnqrk,bhrnkd->bhnqd', p.astype(vr.dtype), vr)
        return o.reshape(b, NA_HEADS, GRID_W, NA_HEAD_DIM)

    o = lax.map(one_row, jnp.arange(rows))
    o = jnp.transpose(o, (1, 0, 3, 2, 4)).reshape(b, t, D_MODEL)
    return o @ out_w + out_b


def depthwise_conv(u, w, bias):
    c = u.shape[-1]
    out = lax.conv_general_dilated(
        u, w.astype(u.dtype)[:, None, :], window_strides=(1,),
        padding=[(SSM_CONV // 2, SSM_CONV // 2)],
        dimension_numbers=('NWC', 'WIO', 'NWC'), feature_group_count=c)
    return out + bias.astype(u.dtype)


def ssd_scan(x, dt, a, bm, cm):
    b, l, nh, p = x.shape
    g, n = bm.shape[2], bm.shape[3]
    j = nh // g
    c = l // SSM_CHUNK
    L = SSM_CHUNK
    xc = (x.astype(jnp.float32) * dt[..., None]).reshape(b, c, L, g, j, p)
    ad = jnp.transpose((dt * a).reshape(b, c, L, g, j), (0, 1, 3, 4, 2))
    a_cs = jnp.cumsum(ad, axis=-1)
    bc = bm.astype(jnp.float32).reshape(b, c, L, g, n)
    cc = cm.astype(jnp.float32).reshape(b, c, L, g, n)
    idx = jnp.arange(L)
    lower = idx[:, None] >= idx[None, :]
    seg = a_cs[..., :, None] - a_cs[..., None, :]
    decay_mat = jnp.exp(jnp.where(lower, seg, -jnp.inf))
    cb = jnp.einsum('bclgn,bcsgn->bcgls', cc, bc)
    y_diag = jnp.einsum('bcgjls,bcsgjp->bclgjp', cb[:, :, :, None] * decay_mat, xc)
    decay_states = jnp.transpose(jnp.exp(a_cs[..., -1:] - a_cs), (0, 1, 4, 2, 3))
    states = jnp.einsum('bcsgn,bcsgjp->bcgjpn', bc, xc * decay_states[..., None])
    chunk_decay = jnp.exp(a_cs[..., -1])

    def step(carry, inp):
        s, d = inp
        return carry * d[..., None, None] + s, carry

    init = jnp.zeros((b, g, j, p, n), jnp.float32)
    _, prev = lax.scan(step, init, (jnp.moveaxis(states, 1, 0), jnp.moveaxis(chunk_decay, 1, 0)))
    prev = jnp.moveaxis(prev, 0, 1)
    in_decay = jnp.transpose(jnp.exp(a_cs), (0, 1, 4, 2, 3))
    y_off = jnp.einsum('bclgn,bcgjpn->bclgjp', cc, prev) * in_decay[..., None]
    return (y_diag + y_off).reshape(b, l, nh, p)


def ssd_mixer(h, in_w, conv_w, conv_b, dt_bias, a_log, d_skip, norm_w, out_w):
    b, t, _ = h.shape
    zxbcdt = h @ in_w
    z = zxbcdt[..., :SSM_D_INNER]
    xbc = zxbcdt[..., SSM_D_INNER:SSM_D_INNER + SSM_CONV_DIM]
    dt = zxbcdt[..., SSM_D_INNER + SSM_CONV_DIM:]
    xbc = jax.nn.silu(depthwise_conv(xbc, conv_w, conv_b))
    xs = xbc[..., :SSM_D_INNER].reshape(b, t, SSM_HEADS, SSM_HEAD_DIM)
    bm = xbc[..., SSM_D_INNER:SSM_D_INNER + SSM_GN].reshape(b, t, SSM_GROUPS, SSM_STATE)
    cm = xbc[..., SSM_D_INNER + SSM_GN:].reshape(b, t, SSM_GROUPS, SSM_STATE)
    dt = jax.nn.softplus(dt.astype(jnp.float32).reshape(b, t, 2, SSM_HEADS)
                         + dt_bias.astype(jnp.float32))
    a = -jnp.exp(a_log.astype(jnp.float32))
    y_f = ssd_scan(xs, dt[:, :, 0], a[0], bm, cm)
    y_b = jnp.flip(ssd_scan(jnp.flip(xs, 1), jnp.flip(dt[:, :, 1], 1), a[1],
                            jnp.flip(bm, 1), jnp.flip(cm, 1)), 1)
    y = y_f + y_b + xs.astype(jnp.float32) * d_skip.astype(jnp.float32)[:, None]
    y = y.reshape(b, t, SSM_D_INNER) * jax.nn.silu(z.astype(jnp.float32))
    yg = y.reshape(b, t, SSM_GROUPS, SSM_D_INNER // SSM_GROUPS)
    yg = yg * lax.rsqrt(jnp.mean(yg * yg, axis=-1, keepdims=True) + RMS_EPS)
    y = yg.reshape(b, t, SSM_D_INNER) * norm_w.astype(jnp.float32)
    return y.astype(h.dtype) @ out_w


def trunk(x, mix_norm, na_qkv_w, na_qkv_b, na_rpb, na_out_w, na_out_b,
          ssm_in_w, ssm_conv_w, ssm_conv_b, ssm_dt_bias, ssm_a_log, ssm_d,
          ssm_norm_w, ssm_out_w, mlp_norm, mlp_up_w, mlp_down_w, final_norm):
    for i in range(DEPTH):
        j = i // N_MIXERS
        hn = rms_norm(x, mix_norm[i])
        if i % N_MIXERS == 0:
            x = x + neighbourhood_attention(hn, na_qkv_w[j], na_qkv_b[j], na_rpb[j],
                                            na_out_w[j], na_out_b[j])
        else:
            x = x + ssd_mixer(hn, ssm_in_w[j], ssm_conv_w[j], ssm_conv_b[j], ssm_dt_bias[j],
                              ssm_a_log[j], ssm_d[j], ssm_norm_w[j], ssm_out_w[j])
        hn = rms_norm(x, mlp_norm[i])
        x = x + jnp.square(jax.nn.relu(hn @ mlp_up_w[i])) @ mlp_down_w[i]
    return rms_norm(x, final_norm)


def setup_inputs(seed: int = 0) -> dict:
    key = jax.random.key(seed)
    ks = jax.random.split(key, 24)
    f32 = jnp.float32
    nrm = lambda k, shape, scale: jax.random.normal(k, shape, f32) * scale
    dt0 = jnp.exp(jax.random.uniform(ks[12], (N_SSM_LAYERS, 2, SSM_HEADS), f32,
                                     np.log(1e-3), np.log(1e-1)))
    return {
        'x_prompt': nrm(ks[0], (BATCH, SEQ, D_MODEL), 1.0),
        'x_sample': nrm(ks[1], (DEC_BATCH, DEC_SEQ, D_MODEL), 1.0),
        'mix_norm': 1.0 + nrm(ks[2], (DEPTH, D_MODEL), 0.01),
        'na_qkv_w': nrm(ks[3], (N_NA_LAYERS, D_MODEL, 3 * D_MODEL), D_MODEL ** -0.5),
        'na_qkv_b': nrm(ks[4], (N_NA_LAYERS, 3 * D_MODEL), 0.01),
        'na_rpb': nrm(ks[5], (N_NA_LAYERS, NA_HEADS, 2 * NA_KH - 1, 2 * NA_KW - 1), 0.1),
        'na_out_w': nrm(ks[6], (N_NA_LAYERS, D_MODEL, D_MODEL), D_MODEL ** -0.5),
        'na_out_b': nrm(ks[7], (N_NA_LAYERS, D_MODEL), 0.01),
        'ssm_in_w': nrm(ks[8], (N_SSM_LAYERS, D_MODEL, SSM_IN_DIM), D_MODEL ** -0.5),
        'ssm_conv_w': nrm(ks[9], (N_SSM_LAYERS, SSM_CONV, SSM_CONV_DIM), SSM_CONV ** -0.5),
        'ssm_conv_b': nrm(ks[10], (N_SSM_LAYERS, SSM_CONV_DIM), 0.01),
        'ssm_dt_bias': dt0 + jnp.log(-jnp.expm1(-dt0)),
        'ssm_a_log': jnp.log(jax.random.uniform(ks[13], (N_SSM_LAYERS, 2, SSM_HEADS), f32, 1.0, 16.0)),
        'ssm_d': 1.0 + nrm(ks[14], (N_SSM_LAYERS, SSM_HEADS), 0.01),
        'ssm_norm_w': 1.0 + nrm(ks[15], (N_SSM_LAYERS, SSM_D_INNER), 0.01),
        'ssm_out_w': nrm(ks[16], (N_SSM_LAYERS, SSM_D_INNER, D_MODEL), SSM_D_INNER ** -0.5),
        'mlp_norm': 1.0 + nrm(ks[17], (DEPTH, D_MODEL), 0.01),
        'mlp_up_w': nrm(ks[18], (DEPTH, D_MODEL, MLP_HIDDEN), D_MODEL ** -0.5),
        'mlp_down_w': nrm(ks[19], (DEPTH, MLP_HIDDEN, D_MODEL), MLP_HIDDEN ** -0.5),
        'final_norm': 1.0 + nrm(ks[20], (D_MODEL,), 0.01),
    }


def reference(x_prompt, x_sample, mix_norm, na_qkv_w, na_qkv_b, na_rpb, na_out_w, na_out_b,
              ssm_in_w, ssm_conv_w, ssm_conv_b, ssm_dt_bias, ssm_a_log, ssm_d,
              ssm_norm_w, ssm_out_w, mlp_norm, mlp_up_w, mlp_down_w, final_norm):
    y_prompt = trunk(x_prompt, mix_norm, na_qkv_w, na_qkv_b, na_rpb, na_out_w, na_out_b,
                     ssm_in_w, ssm_conv_w, ssm_conv_b, ssm_dt_bias, ssm_a_log, ssm_d,
                     ssm_norm_w, ssm_out_w, mlp_norm, mlp_up_w, mlp_down_w, final_norm)
    y_sample = trunk(x_sample, mix_norm, na_qkv_w, na_qkv_b, na_rpb, na_out_w, na_out_b,
                     ssm_in_w, ssm_conv_w, ssm_conv_b, ssm_dt_bias, ssm_a_log, ssm_d,
                     ssm_norm_w, ssm_out_w, mlp_norm, mlp_up_w, mlp_down_w, final_norm)
    return (y_prompt, y_sample)
```

```python
import numpy as np
from contextlib import ExitStack
import concourse.bass as bass
import concourse.mybir as mybir
from concourse.bass_utils import run_bass_kernel_spmd

F32 = mybir.dt.float32
BF16 = mybir.dt.bfloat16
ALU = mybir.AluOpType
AF = mybir.ActivationFunctionType

D = 2048
HID = 8192
EPS = 1e-5


class Prog:
    ENG = ("pe", "act", "dve", "pool", "sp")

    def __init__(self, nc, es):
        self.nc, self.es = nc, es
        self.recs = {e: [] for e in self.ENG}
        self.cnt = {e: 0 for e in self.ENG}
        self.sem = {}
        self.dma_tot = {}
        self.lastw, self.readers = {}, {}
        self.seen = {e: {} for e in self.ENG}

    def _sem(self, key):
        if key not in self.sem:
            self.sem[key] = self.es.enter_context(self.nc.semaphore("s%d" % len(self.sem)))
        return self.sem[key]

    def op(self, eng, fn, reads=(), writes=(), slot=None, embed=True):
        deps = []
        for r in reads:
            t = self.lastw.get(r)
            if t:
                deps.append(t)
        for w in writes:
            t = self.lastw.get(w)
            if t:
                deps.append(t)
            deps += self.readers.get(w, [])
        if slot is None:
            self.cnt[eng] += 1
            tok = (("c", eng), self.cnt[eng])
            inc = (("c", eng), 1)
        else:
            self.dma_tot[slot] = self.dma_tot.get(slot, 0) + 1
            tok = (("d", slot), 16 * self.dma_tot[slot])
            inc = (("d", slot), 16)
        self._sem(inc[0])
        waits = {}
        for (k, v) in deps:
            if k[0] == "d":
                v = 16 * self.dma_tot[k[1]]
                if k == inc[0]:
                    v -= 16
                    if v <= 0:
                        continue
            if k == ("c", "pe") and eng == "pe" and slot is None:
                continue
            if self.seen[eng].get(k, 0) >= v:
                continue
            waits[k] = max(waits.get(k, 0), v)
        for k, v in waits.items():
            self.seen[eng][k] = v
        self.recs[eng].append((fn, list(waits.items()), inc, embed))
        for r in reads:
            self.readers.setdefault(r, []).append(tok)
        for w in writes:
            self.lastw[w] = tok
            self.readers[w] = []

    def finish(self, slots):
        w = [(("d", s), 16 * self.dma_tot[s]) for s in slots if s in self.dma_tot]
        self.recs["sp"].append((None, w, None, False))

    def emit(self):
        def run(name, e):
            for fn, waits, inc, embed in self.recs[name]:
                if fn is None or not embed:
                    for k, v in waits:
                        e.wait_ge(self.sem[k], v)
                    if fn is None:
                        continue
                    waits = []
                for k, v in waits[1:]:
                    e.wait_ge(self.sem[k], v)
                ins = fn(e)
                if waits:
                    ins._wait_ge(self.sem[waits[0][0]], waits[0][1])
                ins.then_inc(self.sem[inc[0]], inc[1])
        with self.nc.Block() as block:
            @block.tensor
            def _(e):
                run("pe", e)

            @block.scalar
            def _(e):
                run("act", e)

            @block.vector
            def _(e):
                run("dve", e)

            @block.gpsimd
            def _(e):
                run("pool", e)

            @block.sync
            def _(e):
                run("sp", e)


class Ctx:
    pass


class RowSplit:
    def __init__(self, a, b, cut):
        self.a, self.b, self.cut = a, b, cut

    def __getitem__(self, idx):
        rs, cs = idx
        assert (rs.start < self.cut) == (rs.stop <= self.cut), (rs, self.cut)
        if rs.stop <= self.cut:
            return self.a[rs, cs]
        return self.b[slice(rs.start - self.cut, rs.stop - self.cut), cs]


ARA = ["arA", "arA2", "arA_q", "arA_k", "arA_v", "arA_o"]
ARB = ["arB", "arB2", "arB_r", "arB_h", "vaug"]
ALLK = {"arA": ARA, "arB": ARB}


def setup(nc, es, T):
    c = Ctx()
    c.nc, c.T = nc, T
    c.P = Prog(nc, es)
    sb = lambda n, s, d: es.enter_context(nc.sbuf_tensor(n, s, d))
    c.arA = sb("arA", [128, 32768], BF16)
    c.arB = sb("arB", [128, 32768], BF16)
    c.NW = 2
    c.wb = [sb("wb%d" % i, [128, 64, 128], BF16) for i in range(c.NW)]
    c.wi = 0
    c.NE = 3
    c.ef = [sb("ef%d" % i, [128, 512], F32) for i in range(c.NE)]
    c.eo = [sb("eo%d" % i, [128, 512], F32) for i in range(c.NE)]
    c.ei = 0
    c.ones = sb("ones", [128, 128], F32)
    c.vecs = sb("vecs", [128, 16 * 16], F32)
    c.NPS = 5
    c.ps = [es.enter_context(nc.psum_tensor("ps%d" % i, [128, 512], F32)) for i in range(7)]
    c.pi = 0
    c.P.op("dve", lambda e: e.memset(c.ones[:], 1.0 / D), writes=["ones"])
    return c


def bf_view(t, off, shape):
    n = int(np.prod(shape[1:]))
    v = t[:, off:off + n]
    if len(shape) == 3:
        v = v.rearrange("p (a b) -> p a b", b=shape[2])
    return v


def f32_view(t, off, shape):
    n = int(np.prod(shape[1:]))
    v = t[:, 2 * off:2 * (off + n)].bitcast(F32)
    if len(shape) == 3:
        v = v.rearrange("p (a b) -> p a b", b=shape[2])
    return v


def load_vec(c, slot_idx, src):
    n = src.shape[0] // 128
    dst = c.vecs[:, slot_idx * 16:slot_idx * 16 + n]
    c.P.op("sp", lambda e: e.dma_start(out=dst, in_=src.rearrange("(k p) -> p k", p=128),
                                       allow_slow_non_contiguous=True),
           writes=[("vec", slot_idx)], slot="ldvec%d" % slot_idx)


def norm_stage(c, xT, wslot, outT, out_dt, xkey, okey):
    P, T = c.P, c.T
    CH = 512
    xin = f32_view(c.arA, 0, [128, 16, CH])
    sq = f32_view(c.arA, 8192, [128, 16, CH])
    rstd = f32_view(c.arB, 0, [128, CH])
    if out_dt == BF16:
        hn = bf_view(c.arB, 2048, [128, 16, CH])
    else:
        hn = f32_view(c.arB, 1024, [128, 16, CH])
    for ci in range(T // CH):
        t0 = ci * CH
        P.op("sp", lambda e, t0=t0: e.dma_start(
            out=xin, in_=xT[:, t0:t0 + CH].rearrange("(k p) t -> p k t", p=128)),
            reads=[(xkey, k, ci) for k in range(16)], writes=ARA, slot="ldA")
        P.op("act", lambda e: e.activation(out=sq, in_=xin, func=AF.Square),
             reads=["arA"], writes=["arA2"])
        ps, pk = next_ps(c)
        for k in range(16):
            P.op("pe", lambda e, k=k, ps=ps: e.matmul(ps[:, :], lhsT=c.ones[:], rhs=sq[:, k, :],
                                                      start=(k == 0), stop=(k == 15)),
                 reads=["arA2", "ones"], writes=[pk])
        P.op("dve", lambda e, ps=ps: e.tensor_scalar(out=rstd, in0=ps[:, :], scalar1=EPS, scalar2=None,
                                                     op0=ALU.add),
             reads=[pk], writes=ARB)
        P.op("act", lambda e: e.activation(out=rstd, in_=rstd, func=AF.Sqrt),
             reads=["arB_r"], writes=["arB_r"])
        P.op("dve", lambda e: e.reciprocal(out=rstd, in_=rstd),
             reads=["arB_r"], writes=["arB_r"])
        for k in range(16):
            P.op("dve", lambda e, k=k: e.scalar_tensor_tensor(
                out=hn[:, k, :], in0=xin[:, k, :], scalar=c.vecs[:, wslot * 16 + k:wslot * 16 + k + 1],
                in1=rstd, op0=ALU.mult, op1=ALU.mult),
                reads=["arA", "arB_r", ("vec", wslot)], writes=["arB_h", "arB"])
        P.op("sp", lambda e, t0=t0: e.dma_start(
            out=outT[:, t0:t0 + CH].rearrange("(k p) t -> p k t", p=128), in_=hn),
            reads=["arB_h"], writes=[(okey, k, ci) for k in range(16)], slot="st_" + okey)


def gemm_F(c, AT, akey, K, W, wcol0, N, epi, groups=None):
    P, T = c.P, c.T
    KT = K // 128
    if K <= 2048:
        TG = min(T, 2048)
        bufs = [(c.arA, "arA"), (c.arB, "arB")]
    else:
        TG = min(T, 65536 // KT // 512 * 512)
        bufs = None
    if groups is None:
        groups = []
        for g in range(T // TG):
            t0 = g * TG
            groups.append((lambda e, t0=t0: AT[:, t0:t0 + TG],
                           [(akey, k, ci) for k in range(KT) for ci in range(t0 // 512, (t0 + TG) // 512)], t0 // 512))
    for g, (ap_fn, rkeys, cbase) in enumerate(groups):
        if bufs is not None:
            ab, abk = bufs[g % 2]
            a_blk = bf_view(ab, 0, [128, KT, TG])
            akeys = [abk]
            P.op("sp", lambda e, a_blk=a_blk, ap_fn=ap_fn: e.dma_start(
                out=a_blk, in_=ap_fn(e).rearrange("(k p) t -> p k t", p=128)),
                reads=rkeys, writes=ALLK[abk], slot="ldA")
        else:
            akeys = ["arA", "arB"]
            half = KT // 2
            for h, (ab, abk) in enumerate([(c.arA, "arA"), (c.arB, "arB")]):
                v = bf_view(ab, 0, [128, half, TG])
                P.op("sp", lambda e, v=v, h=h, ap_fn=ap_fn: e.dma_start(
                    out=v, in_=ap_fn(e)[h * half * 128:(h + 1) * half * 128, :].rearrange("(k p) t -> p k t", p=128)),
                    reads=rkeys, writes=ALLK[abk], slot="ldA")

            def a_of(k, t1, half=half, TG=TG):
                ab = c.arA if k < half else c.arB
                return bf_view(ab, 0, [128, half, TG])[:, k % half, t1:t1 + 512]
        for f in range(N // 128):
            wi = c.wi % c.NW
            c.wi += 1
            wb, wk = c.wb[wi], ("wb", wi)
            col = wcol0 + f * 128
            for k0 in range(0, KT, 16):
                k1 = min(KT, k0 + 16)
                P.op("pool", lambda e, wb=wb, k0=k0, k1=k1, col=col: e.dma_start(
                    out=wb[:, k0:k1, :],
                    in_=W[k0 * 128:k1 * 128, col:col + 128].rearrange("(k p) n -> p k n", p=128)),
                    writes=[wk], slot="ldw%d" % wi)
            for ch in range(TG // 512):
                ps, pk = next_ps(c)
                for k in range(KT):
                    if bufs is not None:
                        rhs = a_blk[:, k, ch * 512:(ch + 1) * 512]
                    else:
                        rhs = a_of(k, ch * 512)
                    P.op("pe", lambda e, ps=ps, wb=wb, k=k, rhs=rhs: e.matmul(
                        ps[:, :], lhsT=wb[:, k, :], rhs=rhs, start=(k == 0), stop=(k == KT - 1)),
                        reads=[wk] + akeys, writes=[pk])
                epi(f, cbase + ch, ps, pk)

def epi_tiles(c):
    i = c.ei % c.NE
    c.ei += 1
    return c.ef[i], ("ef", i), c.eo[i], ("eo", i)


def mlp_stage(c, hnT, up_w, hT, down_w, xT):
    P = c.P

    def epi_up(f, ci, ps, pk):
        ef, efk, eo, eok = epi_tiles(c)
        ob = eo[:, 0:256].bitcast(BF16)
        P.op("dve", lambda e: e.tensor_scalar(out=ef[:], in0=ps[:, :], scalar1=0.0, scalar2=None,
                                              op0=ALU.max), reads=[pk], writes=[efk])
        P.op("act", lambda e: e.activation(out=ob, in_=ef[:], func=AF.Square),
             reads=[efk], writes=[eok])
        P.op("sp", lambda e: e.dma_start(out=hT[f * 128:(f + 1) * 128, ci * 512:(ci + 1) * 512], in_=ob),
             reads=[eok], writes=[("hT", f, ci)], slot="st_hT")

    gemm_F(c, hnT, "hnT", D, up_w, 0, HID, epi_up)
    gemm_F(c, hT, "hT", HID, down_w, 0, D, lambda f, ci, ps, pk: epi_resid(c, xT, None, f, ci, ps, pk))


def epi_resid(c, xT, bias_slot, f, ci, ps, pk):
    P = c.P
    ef, efk, eo, eok = epi_tiles(c)
    xs = xT[f * 128:(f + 1) * 128, ci * 512:(ci + 1) * 512]
    P.op("sp", lambda e: e.dma_start(out=ef[:], in_=xs), reads=[("xT", f, ci)], writes=[efk], slot="ldx")
    if bias_slot is None:
        P.op("dve", lambda e: e.tensor_tensor(out=eo[:], in0=ps[:, :], in1=ef[:], op=ALU.add),
             reads=[pk, efk], writes=[eok])
    else:
        b = c.vecs[:, bias_slot * 16 + f:bias_slot * 16 + f + 1]
        P.op("dve", lambda e: e.scalar_tensor_tensor(out=eo[:], in0=ps[:, :], scalar=b, in1=ef[:],
                                                     op0=ALU.add, op1=ALU.add),
             reads=[pk, efk, ("vec", bias_slot)], writes=[eok])
    P.op("sp", lambda e: e.dma_start(out=xs, in_=eo[:]), reads=[eok], writes=[("xT", f, ci)], slot="st_xT")


def build_mlp_only(T):
    nc = bass.Bass("TRN2", target_bir_lowering=False)
    xin = nc.dram_tensor("xT_in", [D, T], F32, kind="ExternalInput").ap()
    nw = nc.dram_tensor("nw", [D], F32, kind="ExternalInput").ap()
    up = nc.dram_tensor("up", [D, HID], F32, kind="ExternalInput").ap()
    dn = nc.dram_tensor("dn", [HID, D], F32, kind="ExternalInput").ap()
    yT = nc.dram_tensor("yT", [D, T], F32, kind="ExternalOutput").ap()
    hnT = nc.dram_tensor("hnT", [D, T], BF16).ap()
    hT = nc.dram_tensor("hT", [HID, T], BF16).ap()
    with ExitStack() as es:
        c = setup(nc, es, T)
        P = c.P
        load_vec(c, 0, nw)
        for ci in range(T // 512):
            for f in range(16):
                P.op("sp", lambda e, f=f, ci=ci: e.dma_start(
                    out=yT[f * 128:(f + 1) * 128, ci * 512:(ci + 1) * 512],
                    in_=xin[f * 128:(f + 1) * 128, ci * 512:(ci + 1) * 512]),
                    writes=[("xT", f, ci)], slot="st_xT")
        norm_stage(c, yT, 0, hnT, BF16, "xT", "hnT")
        mlp_stage(c, hnT, up, hT, dn, yT)
        P.finish(["st_xT"])
        P.emit()
    return nc


NEG = -30000.0


def na_windows(R, brk):
    wins = []
    for r in range(R):
        rsA = int(np.clip(r - 4, 0, R - 8))
        lo, hi = rsA, rsA + 8
        if brk is not None and brk - 4 <= r < brk + 4:
            rsB = int(np.clip(r - 4, 0, brk - 8)) if r < brk else int(np.clip(r - 4, brk, R - 8))
            lo, hi = min(rsA, rsB), max(rsA, rsB) + 8
        wins.append(list(range(lo, hi)))
    return wins


def na_pen_table(R, brk, cont):
    pen = np.zeros((8, 12), np.float32)
    if brk is None:
        return pen
    wins = na_windows(R, brk)
    for i, r in enumerate(range(brk - 4, brk + 4)):
        rsA = int(np.clip(r - 4, 0, R - 8))
        rsB = int(np.clip(r - 4, 0, brk - 8)) if r < brk else int(np.clip(r - 4, brk, R - 8))
        rs = rsA if cont else rsB
        for j, rr in enumerate(wins[r]):
            pen[i, j] = 0.0 if rs <= rr < rs + 8 else NEG
    return pen


def na_bias_table(rpb_l):
    kc = np.arange(64)[:, None]
    qc = np.arange(64)[None, :]
    cs = np.clip(qc - 8, 0, 48)
    valid = (kc >= cs) & (kc < cs + 16)
    dc = np.clip(kc - qc, -15, 15) + 15
    g = rpb_l[:, :, dc]
    g = np.where(valid[None, None], g, np.float32(NEG)).astype(np.float32)
    g = g.reshape(16, 2, 15, 64, 64).transpose(0, 3, 1, 2, 4)
    return np.ascontiguousarray(g)


def na_setup(c, es):
    nc = c.nc
    sb = lambda n, s, d: es.enter_context(nc.sbuf_tensor(n, s, d))
    c.ident = sb("ident", [128, 128], BF16)
    c.identf = sb("identf", [128, 128], F32)
    c.sT = [sb("sT%d" % i, [64, 512], F32) for i in range(2)]
    c.pT = [sb("pT%d" % i, [64, 512], BF16) for i in range(2)]
    c.osb = [sb("osb%d" % i, [64, 3, 2, 64], BF16) for i in range(2)]
    c.rec = [sb("rec%d" % i, [64, 8], F32) for i in range(2)]
    c.pen = sb("pen", [64, 96], F32)
    c.nbias = sb("nbias", [64, 2 * 15 * 64], F32)
    c.psb = es.enter_context(nc.psum_tensor("psb", [128, 1024], BF16))
    c.si = c.oi = c.pob = 0


def load_ident(c, ident_d, identf_d):
    c.P.op("sp", lambda e: e.dma_start(out=c.ident[:], in_=ident_d), writes=["ident"], slot="ld_ident")
    c.P.op("sp", lambda e: e.dma_start(out=c.identf[:], in_=identf_d), writes=["identf"], slot="ld_identf")


def next_ps(c):
    i = c.pi % c.NPS
    c.pi += 1
    return c.ps[i], ("ps", i)


def na_core(c, qkvT, oT, nabias_l, R, brk, seqs=None, npairs=16, qstride=D):
    if seqs is None:
        seqs = [(0, R)]
    for pair in range(npairs):
        for (tok0, Rs) in seqs:
            _na_pair_seq(c, qkvT, oT, nabias_l, pair, tok0, Rs, brk, qstride)


def _na_pair_seq(c, qkvT, oT, nabias_l, pair, tok0, R, brk, qstride):
    P, T = c.P, R * 64
    wins = na_windows(R, brk)
    qT = bf_view(c.arA, 0, [128, T])
    kT = bf_view(c.arA, 8192, [128, T])
    vT = bf_view(c.arA, 16384, [128, T])
    oTp = bf_view(c.arA, 24576, [128, T])
    vaug = c.arB[0:64, 0:R * 130].rearrange("p (r h d) -> p r h d", h=2, d=65)
    P.op("dve", lambda e: e.memset(c.arB[0:64, 0:R * 130], 1.0), writes=ARB)
    for i, (dst, key) in enumerate([(qT, "arA_q"), (kT, "arA_k"), (vT, "arA_v")]):
        r0 = i * qstride + pair * 128
        P.op("sp", lambda e, dst=dst, r0=r0: e.dma_start(out=dst, in_=qkvT[r0:r0 + 128, tok0:tok0 + T]),
             reads=[("qkvT", r0 // 128, ci) for ci in range(tok0 // 512, (tok0 + T) // 512)],
             writes=(ARA if i == 0 else [key]), slot="ldA")
    P.op("sp", lambda e, pair=pair: e.dma_start(
        out=c.nbias[:], in_=nabias_l[pair].rearrange("k h d q -> k (h d q)")),
        writes=["nbias"], slot="ldnb")
    for r8 in range(0, R, 8):
        n8 = min(8, R - r8)
        for j in range(n8):
            rr = r8 + j
            P.op("pe", lambda e, rr=rr, j=j: e.transpose(
                out=c.psb[0:64, j * 128:(j + 1) * 128], in_=vT[:, rr * 64:(rr + 1) * 64],
                identity=c.ident[:]), reads=["arA_v", "ident"], writes=["psb"])
        P.op("act", lambda e, r8=r8, n8=n8: e.activation(
            out=vaug[:, r8:r8 + n8, :, 0:64],
            in_=c.psb[0:64, 0:n8 * 128].rearrange("p (r h d) -> p r h d", h=2, d=64), func=AF.Copy),
            reads=["psb"], writes=["vaug"])
    po, pok = None, None
    pend = []

    def flush(pend, po, pok):
        oi = c.oi % 2
        c.oi += 1
        osb, rec = c.osb[oi], c.rec[oi]
        n = len(pend)
        pov = po[0:64, 0:n * 130].rearrange("p (r h d) -> p r h d", h=2, d=65)
        P.op("dve", lambda e: e.reciprocal(
            out=rec[:, 0:2 * n], in_=po[0:64, 0:n * 130].rearrange("p (g d) -> p g d", d=65)[:, :, 64]),
            reads=[pok], writes=[("rec", oi)])
        for i in range(n):
            for hh in range(2):
                P.op("dve", lambda e, i=i, hh=hh: e.tensor_scalar(
                    out=osb[:, i, hh, :], in0=pov[:, i, hh, 0:64],
                    scalar1=rec[:, 2 * i + hh:2 * i + hh + 1], scalar2=None, op0=ALU.mult),
                    reads=[pok, ("rec", oi)], writes=[("osb", oi)])
        for i, r in enumerate(pend):
            P.op("pe", lambda e, i=i: e.transpose(
                out=c.psb[:, i * 64:(i + 1) * 64], in_=osb[:, i, :, :].rearrange("p h d -> p (h d)"),
                identity=c.ident[0:64, 0:64]), reads=[("osb", oi), "ident"], writes=["psb"])
        r0 = pend[0]
        P.op("act", lambda e: e.activation(out=oTp[:, r0 * 64:(r0 + n) * 64],
                                           in_=c.psb[:, 0:n * 64], func=AF.Copy),
             reads=["psb"], writes=["arA_o"])

    for r in range(R):
        if not pend:
            po, pok = c.ps[5 + c.pob % 2], ("ps", 5 + c.pob % 2)
            c.pob += 1
        slot_i = len(pend)
        for hh in range(2):
            hs = slice(hh * 64, (hh + 1) * 64)
            win = wins[r]
            groups = [win[a:a + 8] for a in range(0, len(win), 8)]
            col = (slot_i * 2 + hh) * 65
            nslot = 0
            for grp in groups:
                ng = len(grp)
                ps, pk = next_ps(c)
                for j, rr in enumerate(grp):
                    P.op("pe", lambda e, ps=ps, j=j, rr=rr, hs=hs, r=r: e.matmul(
                        ps[0:64, j * 64:(j + 1) * 64], lhsT=kT[hs, rr * 64:(rr + 1) * 64],
                        rhs=qT[hs, r * 64:(r + 1) * 64], start=True, stop=True),
                        reads=["arA_k", "arA_q"], writes=[pk])
                si = c.si % 2
                c.si += 1
                sT, pT = c.sT[si], c.pT[si]
                d0 = grp[0] - r + 7
                assert 0 <= d0 and d0 + ng <= 15
                b0 = (hh * 15 + d0) * 64
                P.op("dve", lambda e, ps=ps, sT=sT, ng=ng, b0=b0: e.scalar_tensor_tensor(
                    out=sT[:, 0:ng * 64], in0=ps[0:64, 0:ng * 64], scalar=0.125,
                    in1=c.nbias[:, b0:b0 + ng * 64], op0=ALU.mult, op1=ALU.add),
                    reads=[pk, "nbias"], writes=[("sT", si)])
                isbrk = brk is not None and brk - 4 <= r < brk + 4
                if not isbrk:
                    P.op("act", lambda e, sT=sT, pT=pT, ng=ng: e.activation(
                        out=pT[:, 0:ng * 64], in_=sT[:, 0:ng * 64], func=AF.Exp),
                        reads=[("sT", si)], writes=[("pT", si)])
                else:
                    for j in range(ng):
                        pc = (r - (brk - 4)) * 12 + nslot + j
                        P.op("act", lambda e, sT=sT, pT=pT, j=j, pc=pc: e.activation(
                            out=pT[:, j * 64:(j + 1) * 64], in_=sT[:, j * 64:(j + 1) * 64], func=AF.Exp,
                            bias=c.pen[:, pc:pc + 1], scale=1.0),
                            reads=[("sT", si), "pen"], writes=[("pT", si)])
                for j, rr in enumerate(grp):
                    first = (nslot + j == 0)
                    last = (nslot + j == len(win) - 1)
                    P.op("pe", lambda e, po=po, pT=pT, j=j, rr=rr, hh=hh, col=col, first=first, last=last:
                         e.matmul(po[0:64, col:col + 65], lhsT=pT[:, j * 64:(j + 1) * 64],
                                  rhs=vaug[:, rr, hh, :], start=first, stop=last),
                         reads=[("pT", si), "vaug"], writes=[pok])
                nslot += ng
        pend.append(r)
        if len(pend) == 3 or r == R - 1:
            flush(pend, po, pok)
            pend = []
    P.op("sp", lambda e, pair=pair: e.dma_start(out=oT[pair * 128:(pair + 1) * 128, tok0:tok0 + T], in_=oTp),
         reads=["arA_o"], writes=[("oT", pair, ci) for ci in range(tok0 // 512, (tok0 + T) // 512)], slot="st_oT")


def na_layer(c, xT, l, li, W, R, brk):
    load_vec(c, 0, W["mix_norm"][li])
    norm_stage(c, xT, 0, W["hnT"], BF16, "xT", "hnT")
    for i in range(3):
        load_vec(c, 1 + i, W["na_qkv_b"][l, i * D:(i + 1) * D])

    def epi_qkv(f, ci, ps, pk):
        ef, efk, eo, eok = epi_tiles(c)
        ob = eo[:, 0:256].bitcast(BF16)
        b = c.vecs[:, (1 + f // 16) * 16 + f % 16:(1 + f // 16) * 16 + f % 16 + 1]
        c.P.op("dve", lambda e: e.tensor_scalar(out=ob, in0=ps[:, :], scalar1=b, scalar2=None, op0=ALU.add),
               reads=[pk, ("vec", 1 + f // 16)], writes=[eok])
        c.P.op("sp", lambda e: e.dma_start(out=W["qkvT"][f * 128:(f + 1) * 128, ci * 512:(ci + 1) * 512], in_=ob),
               reads=[eok], writes=[("qkvT", f, ci)], slot="st_qkv")

    gemm_F(c, W["hnT"], "hnT", D, W["na_qkv_w"][l], 0, 3 * D, epi_qkv)
    na_core(c, W["qkvT"], W["oT"], W["nabias"][l], R, brk)
    load_vec(c, 4, W["na_out_b"][l])
    gemm_F(c, W["oT"], "oT", D, W["na_out_w"][l], 0, D,
           lambda f, ci, ps, pk: epi_resid(c, xT, 4, f, ci, ps, pk))


def build_na_only(T, brk):
    nc = bass.Bass("TRN2", target_bir_lowering=False)
    R = T // 64
    ext = lambda n, s, d=F32: nc.dram_tensor(n, s, d, kind="ExternalInput").ap()
    xin = ext("xT_in", [D, T])
    W = {
        "mix_norm": ext("mix_norm", [1, D]), "na_qkv_w": ext("na_qkv_w", [1, D, 3 * D]),
        "na_qkv_b": ext("na_qkv_b", [1, 3 * D]), "na_out_w": ext("na_out_w", [1, D, D]),
        "na_out_b": ext("na_out_b", [1, D]), "nabias": ext("nabias", [1, 16, 64, 2, 15, 64]),
    }
    pen_d = ext("pen_in", [64, 96])
    ident_d = ext("ident_in", [128, 128], BF16)
    identf_d = ext("identf_in", [128, 128])
    yT = nc.dram_tensor("yT", [D, T], F32, kind="ExternalOutput").ap()
    W["hnT"] = nc.dram_tensor("hnT", [D, T], BF16).ap()
    W["qkvT"] = nc.dram_tensor("qkvT", [3 * D, T], BF16).ap()
    W["oT"] = nc.dram_tensor("oT", [D, T], BF16).ap()
    with ExitStack() as es:
        c = setup(nc, es, T)
        na_setup(c, es)
        P = c.P
        load_ident(c, ident_d, identf_d)
        P.op("sp", lambda e: e.dma_start(out=c.pen[:], in_=pen_d), writes=["pen"], slot="ld_pen")
        for ci in range(T // 512):
            for f in range(16):
                P.op("sp", lambda e, f=f, ci=ci: e.dma_start(
                    out=yT[f * 128:(f + 1) * 128, ci * 512:(ci + 1) * 512],
                    in_=xin[f * 128:(f + 1) * 128, ci * 512:(ci + 1) * 512]),
                    writes=[("xT", f, ci)], slot="st_xT")
        na_layer(c, yT, 0, 0, W, R, brk)
        P.finish(["st_xT"])
        P.emit()
    return nc


DI = 4096


def ssd_setup(c, es):
    nc = c.nc
    sb = lambda n, s, d: es.enter_context(nc.sbuf_tensor(n, s, d))
    c.ucum = sb("ucum", [128, 128], F32)
    c.lcum = sb("lcum", [128, 128], F32)
    c.ones1 = sb("ones1", [128, 128], F32)
    c.onesg = sb("onesg", [128, 128], F32)
    c.flag = sb("flag", [128, 1], F32)
    c.cw = sb("cw", [128, 6, 5], F32)
    c.cb = sb("cbias", [128, 6], F32)
    c.dtb = sb("dtb", [32, 1], F32)
    c.sc = sb("sc", [32, 1], F32)
    c.dsk = sb("dsk", [128, 512], F32)
    c.nwg = sb("nwg", [128, 4], F32)
    c.carry = sb("carry", [128, 512], F32)


def ssd_consts(c, ucum_d, lcum_d, flag_d):
    P = c.P
    P.op("sp", lambda e: e.dma_start(out=c.ucum[:], in_=ucum_d), writes=["ucum"], slot="ld_ucum")
    P.op("sp", lambda e: e.dma_start(out=c.lcum[:], in_=lcum_d), writes=["lcum"], slot="ld_lcum")
    P.op("sp", lambda e: e.dma_start(out=c.flag[:], in_=flag_d), writes=["flag"], slot="ld_flag")
    P.op("dve", lambda e: e.memset(c.ones1[:], 1.0), writes=["ones1"])
    P.op("dve", lambda e: e.memset(c.onesg[:], 1.0 / 512), writes=["onesg"])


def ssd_core(c, zxT, ysT, W, l, nch, brk_ch, lay=None, seqs=None):
    P = c.P
    A = c.arA
    if lay is None:
        lay = dict(G=8, zb=0, xb=4096, bb=8192, cb=9216, dtb=10240, dtd=64, gs=1, ccB=4096, ccC=5120,
                   dsk=lambda g: W["dskip"][l, g])
    if seqs is None:
        seqs = [(0, nch)]
    gs = lay["gs"]
    xin = f32_view(A, 0, [128, 6, 132])
    xs = f32_view(A, 800, [128, 6, 128])
    xtm = f32_view(A, 1600, [128, 512])
    xcf = f32_view(A, 2112, [128, 512])
    seg = f32_view(A, 2624, [128, 1024])
    er = f32_view(A, 3648, [128, 1024])
    adu = f32_view(A, 4672, [128, 1024])
    cbm = f32_view(A, 5696, [128, 128])
    ddt = f32_view(A, 5824, [128, 32])
    colv = f32_view(A, 5856, [128, 32])
    y1 = f32_view(A, 5888, [128, 512])
    yT = f32_view(A, 6400, [128, 4, 128])
    zt = f32_view(A, 6912, [128, 4, 128])
    sqg = f32_view(A, 7424, [128, 4, 128])
    rs = f32_view(A, 7936, [128, 128])
    dd = f32_view(A, 8064, [128, 128])
    dtr = f32_view(A, 8192, [128, 128])
    bo = 2 * 8320
    xcb = [bf_view(A, bo, [128, 512]), bf_view(A, bo + 4736, [128, 512])]
    xdb = bf_view(A, bo + 512, [128, 512])
    mb = [bf_view(A, bo + 1024, [128, 1024]), bf_view(A, bo + 5248, [128, 1024])]
    csb = [bf_view(A, bo + 2048, [128, 1024]), bf_view(A, bo + 6272, [128, 1024])]
    btm = bf_view(A, bo + 3072, [128, 128])
    btb = bf_view(A, bo + 3200, [128, 128])
    ctb = bf_view(A, bo + 3328, [128, 128])
    ctf = f32_view(A, (bo + 3456) // 2, [128, 128])
    prb = bf_view(A, bo + 3712, [128, 512])
    yob = bf_view(A, bo + 4224, [128, 4, 128])
    prevf = c.arB
    K_ = "ssd"

    def sk(n):
        return ("ssd", n)

    def group(g, c0, ncs):
        for j in range(6):
            ch = (512 * g * gs + 128 * j) if j < 4 else (lay["ccB"] + 128 * g * gs if j == 4 else lay["ccC"] + 128 * g * gs)
            P.op("sp", lambda e, j=j, ch=ch: e.dma_start(
                out=c.cw[:, j, :], in_=W["ssm_conv_w"][l][:, ch:ch + 128].rearrange("k p -> p k"),
                allow_slow_non_contiguous=True), writes=["cw"], slot="ld_cw")
            P.op("sp", lambda e, j=j, ch=ch: e.dma_start(
                out=c.cb[:, j:j + 1], in_=W["ssm_conv_b"][l][ch:ch + 128].rearrange("(p o) -> p o", o=1)),
                writes=["cb"], slot="ld_cb")
        P.op("dve", lambda e: e.memset(c.sc[0:32, :], 0.0), writes=["sc"])
        for d in range(2):
            for rep in range(2):
                p0 = rep * 16 + d * 8
                P.op("sp", lambda e, d=d, p0=p0: e.dma_start(
                    out=c.dtb[p0:p0 + 8, :], in_=W["ssm_dt_bias"][l][d, 8 * g * gs:8 * g * gs + 8].rearrange("(p o) -> p o", o=1)),
                    writes=["dtb"], slot="ld_dtb")
            P.op("sp", lambda e, d=d: e.dma_start(
                out=c.sc[16 + d * 8:24 + d * 8, :], in_=W["ssm_a_log"][l][d, 8 * g * gs:8 * g * gs + 8].rearrange("(p o) -> p o", o=1)),
                writes=["sc"], slot="ld_sc")
        P.op("act", lambda e: e.activation(out=c.sc[0:32, :], in_=c.sc[0:32, :], func=AF.Exp), reads=["sc"], writes=["sc"])
        P.op("dve", lambda e: e.tensor_scalar(out=c.sc[0:32, :], in0=c.sc[0:32, :], scalar1=-1.0, scalar2=None, op0=ALU.mult),
             reads=["sc"], writes=["sc"])
        P.op("dve", lambda e: e.memset(c.sc[0:16, :], 1.0), reads=["sc"], writes=["sc"])
        P.op("sp", lambda e, g=g: e.dma_start(out=c.dsk[:], in_=lay["dsk"](g)), writes=["dsk"], slot="ld_dsk")
        P.op("sp", lambda e, g=g: e.dma_start(
            out=c.nwg[:], in_=W["ssm_norm_w"][l][512 * g * gs:512 * g * gs + 512].rearrange("(k p) -> p k", p=128),
            allow_slow_non_contiguous=True), writes=["nwg"], slot="ld_nwg")

        def prep(ci, full):
            t0 = ci * 128
            lo, hi = max(t0 - 2, c0 * 128), min(t0 + 130, (c0 + ncs) * 128)
            P.op("dve", lambda e: e.memset(xin, 0.0), writes=[K_])
            for j in range(6 if full else 5):
                row = (lay["xb"] + 512 * g * gs + 128 * j) if j < 4 else (lay["bb"] + 128 * g * gs if j == 4 else lay["cb"] + 128 * g * gs)
                P.op("sp", lambda e, j=j, row=row: e.dma_start(
                    out=xin[:, j, lo - (t0 - 2):hi - (t0 - 2)], in_=zxT[row:row + 128, lo:hi]),
                    reads=[("zxT", row // 128, q) for q in range(lo // 512, (hi - 1) // 512 + 1)],
                    writes=[K_], slot="ldS")
            nj = 6 if full else 5
            if brk_ch is not None and ci == brk_ch - 1:
                P.op("dve", lambda e: e.tensor_scalar(out=xin[:, 0:nj, 130:132], in0=xin[:, 0:nj, 130:132],
                                                      scalar1=c.flag[:, 0:1], scalar2=None, op0=ALU.mult),
                     reads=[K_, "flag"], writes=[K_])
            if brk_ch is not None and ci == brk_ch:
                P.op("dve", lambda e: e.tensor_scalar(out=xin[:, 0:nj, 0:2], in0=xin[:, 0:nj, 0:2],
                                                      scalar1=c.flag[:, 0:1], scalar2=None, op0=ALU.mult),
                     reads=[K_, "flag"], writes=[K_])
            for j in range(nj):
                P.op("dve", lambda e, j=j: e.tensor_scalar(
                    out=xs[:, j, :], in0=xin[:, j, 0:128], scalar1=c.cw[:, j, 0:1], scalar2=c.cb[:, j:j + 1],
                    op0=ALU.mult, op1=ALU.add), reads=[K_, "cw", "cb"], writes=[K_])
                for k in range(1, 5):
                    P.op("dve", lambda e, j=j, k=k: e.scalar_tensor_tensor(
                        out=xs[:, j, :], in0=xin[:, j, k:k + 128], scalar=c.cw[:, j, k:k + 1], in1=xs[:, j, :],
                        op0=ALU.mult, op1=ALU.add), reads=[K_, "cw"], writes=[K_])
            P.op("act", lambda e: e.activation(out=xs[:, 0:nj, :], in_=xs[:, 0:nj, :], func=AF.Silu),
                 reads=[K_], writes=[K_])
            ps, pk = next_ps(c)
            for j in range(4):
                P.op("pe", lambda e, ps=ps, j=j: e.transpose(out=ps[:, j * 128:(j + 1) * 128], in_=xs[:, j, :],
                                                             identity=c.identf[:]), reads=[K_, "identf"], writes=[pk])
            P.op("act", lambda e, ps=ps: e.activation(out=xtm, in_=ps[:, :], func=AF.Copy), reads=[pk], writes=[K_])
            ps, pk = next_ps(c)
            P.op("pe", lambda e, ps=ps: e.transpose(out=ps[:, 0:128], in_=xs[:, 4, :], identity=c.identf[:]),
                 reads=[K_, "identf"], writes=[pk])
            P.op("act", lambda e, ps=ps: e.activation(out=btm, in_=ps[:, 0:128], func=AF.Copy), reads=[pk], writes=[K_])
            P.op("dve", lambda e: e.tensor_copy(out=btb, in_=xs[:, 4, :]), reads=[K_], writes=[K_])
            if full:
                P.op("dve", lambda e: e.tensor_copy(out=ctb, in_=xs[:, 5, :]), reads=[K_], writes=[K_])
                P.op("dve", lambda e: e.tensor_copy(out=ctf, in_=xs[:, 5, :]), reads=[K_], writes=[K_])
            for d in range(2):
                for rep in range(2):
                    p0 = rep * 16 + d * 8
                    row = lay["dtb"] + lay["dtd"] * d + 8 * g * gs
                    P.op("sp", lambda e, p0=p0, row=row: e.dma_start(out=dtr[p0:p0 + 8, :], in_=zxT[row:row + 8, t0:t0 + 128]),
                         reads=[("zxT", lay["dtb"] // 128, t0 // 512)], writes=[K_], slot="ldS")
            P.op("act", lambda e: e.activation(out=dd[0:32, :], in_=dtr[0:32, :], func=AF.Exp, bias=c.dtb[0:32, :], scale=1.0),
                 reads=[K_, "dtb"], writes=[K_])
            P.op("dve", lambda e: e.tensor_scalar(out=dd[0:32, :], in0=dd[0:32, :], scalar1=1.0, scalar2=None, op0=ALU.add),
                 reads=[K_], writes=[K_])
            P.op("act", lambda e: e.activation(out=dd[0:32, :], in_=dd[0:32, :], func=AF.Ln), reads=[K_], writes=[K_])
            P.op("dve", lambda e: e.tensor_scalar(out=dd[0:32, :], in0=dd[0:32, :], scalar1=c.sc[0:32, :], scalar2=None,
                                                  op0=ALU.mult), reads=[K_, "sc"], writes=[K_])
            ps, pk = next_ps(c)
            P.op("pe", lambda e, ps=ps: e.transpose(out=ps[:, 0:32], in_=dd[0:32, :], identity=c.identf[0:32, 0:32]),
                 reads=[K_, "identf"], writes=[pk])
            P.op("act", lambda e, ps=ps: e.activation(out=ddt, in_=ps[:, 0:32], func=AF.Copy), reads=[pk], writes=[K_])

        def direction(ci, d, want_y, py, pyk, first_dir, prev_ap, upd=True):
            cum = c.ucum if d == 0 else c.lcum
            ck = "ucum" if d == 0 else "lcum"
            adc = ddt[:, 16 + 8 * d:24 + 8 * d]
            dtc = ddt[:, 8 * d:8 * d + 8]
            ps, pk = next_ps(c)
            P.op("pe", lambda e, ps=ps: e.matmul(ps[:, 0:8], lhsT=cum[:], rhs=adc, start=True, stop=True),
                 reads=[K_, ck], writes=[pk])
            P.op("pe", lambda e, ps=ps: e.matmul(ps[:, 8:16], lhsT=c.ones1[:], rhs=adc, start=True, stop=True),
                 reads=[K_, "ones1"], writes=[pk])
            P.op("act", lambda e, ps=ps: e.activation(out=colv[:, 0:16], in_=ps[:, 0:16], func=AF.Copy), reads=[pk], writes=[K_])
            P.op("dve", lambda e: e.tensor_tensor(out=colv[:, 16:24], in0=colv[:, 8:16], in1=colv[:, 0:8], op=ALU.subtract),
                 reads=[K_], writes=[K_])
            P.op("act", lambda e: e.activation(out=colv[:, 16:24], in_=colv[:, 16:24], func=AF.Exp), reads=[K_], writes=[K_])
            P.op("act", lambda e: e.activation(out=colv[:, 24:32], in_=colv[:, 8:16], func=AF.Exp), reads=[K_], writes=[K_])
            for h in range(8):
                P.op("dve", lambda e, h=h: e.tensor_scalar(out=xcf[:, h * 64:(h + 1) * 64], in0=xtm[:, h * 64:(h + 1) * 64],
                                                           scalar1=dtc[:, h:h + 1], scalar2=None, op0=ALU.mult),
                     reads=[K_], writes=[K_])
            for h in range(8):
                P.op("dve", lambda e, h=h: e.tensor_scalar(out=xdb[:, h * 64:(h + 1) * 64], in0=xcf[:, h * 64:(h + 1) * 64],
                                                           scalar1=colv[:, 16 + h:17 + h], scalar2=None, op0=ALU.mult),
                     reads=[K_], writes=[K_])
            if want_y:
                P.op("dve", lambda e: e.tensor_copy(out=xcb[d], in_=xcf), reads=[K_], writes=[K_])
                for h in range(8):
                    P.op("dve", lambda e, h=h: e.tensor_scalar(out=adu[:, h * 128:(h + 1) * 128], in0=cum[:],
                                                               scalar1=adc[:, h:h + 1], scalar2=None, op0=ALU.mult),
                         reads=[K_, ck], writes=[K_])
                for half in range(2):
                    pr, prk = next_ps(c)
                    P.op("pe", lambda e, pr=pr, half=half: e.matmul(pr[:, :], lhsT=c.ones1[:], rhs=adu[:, half * 512:(half + 1) * 512],
                                                                    start=True, stop=True), reads=[K_, "ones1"], writes=[prk])
                    for hh in range(4):
                        h = half * 4 + hh
                        P.op("dve", lambda e, pr=pr, h=h, hh=hh: e.tensor_scalar(
                            out=seg[:, h * 128:(h + 1) * 128], in0=pr[:, hh * 128:(hh + 1) * 128],
                            scalar1=colv[:, h:h + 1], scalar2=0.0, op0=ALU.subtract, op1=ALU.min),
                            reads=[prk, K_], writes=[K_])
                    P.op("act", lambda e, pr=pr, half=half: e.activation(out=er[:, half * 512:(half + 1) * 512], in_=pr[:, :], func=AF.Exp),
                         reads=[prk], writes=[K_])
                P.op("act", lambda e: e.activation(out=seg, in_=seg, func=AF.Exp), reads=[K_], writes=[K_])
                pc, pck = next_ps(c)
                P.op("pe", lambda e, pc=pc: e.matmul(pc[:, 0:128], lhsT=btb, rhs=ctb, start=True, stop=True), reads=[K_], writes=[pck])
                P.op("dve", lambda e, pc=pc: e.tensor_tensor(out=cbm, in0=pc[:, 0:128], in1=cum[:], op=ALU.mult),
                     reads=[pck, ck], writes=[K_])
                for h in range(8):
                    P.op("dve", lambda e, h=h: e.tensor_tensor(out=mb[d][:, h * 128:(h + 1) * 128], in0=seg[:, h * 128:(h + 1) * 128],
                                                               in1=cbm, op=ALU.mult), reads=[K_], writes=[K_])
                    P.op("dve", lambda e, h=h: e.tensor_tensor(out=csb[d][:, h * 128:(h + 1) * 128], in0=er[:, h * 128:(h + 1) * 128],
                                                               in1=ctf, op=ALU.mult), reads=[K_], writes=[K_])
            if not upd:
                return
            pS, pSk = next_ps(c)
            P.op("pe", lambda e, pS=pS: e.matmul(pS[:, :], lhsT=btm, rhs=xdb, start=True, stop=True), reads=[K_], writes=[pSk])
            for h in range(8):
                P.op("dve", lambda e, h=h: e.tensor_scalar(out=c.carry[:, h * 64:(h + 1) * 64], in0=c.carry[:, h * 64:(h + 1) * 64],
                                                           scalar1=colv[:, 24 + h:25 + h], scalar2=None, op0=ALU.mult),
                     reads=[K_, "carry"], writes=["carry"])
            P.op("dve", lambda e, pS=pS: e.tensor_tensor(out=c.carry[:], in0=c.carry[:], in1=pS[:, :], op=ALU.add),
                 reads=[pSk, "carry"], writes=["carry"])

        P.op("dve", lambda e: e.memset(c.carry[:], 0.0), writes=["carry"])
        for ci in range(c0, c0 + ncs):
            if brk_ch is not None and ci == brk_ch:
                P.op("dve", lambda e: e.tensor_scalar(out=c.carry[:], in0=c.carry[:], scalar1=c.flag[:, 0:1], scalar2=None,
                                                      op0=ALU.mult), reads=["carry", "flag"], writes=["carry"])
            P.op("dve", lambda e, ci=ci: e.tensor_copy(out=prevf[:, (ci - c0) * 512:(ci - c0 + 1) * 512], in_=c.carry[:]),
                 reads=["carry"], writes=ARB + ["prev"])
            prep(ci, False)
            direction(ci, 0, False, None, None, True, None)
        P.op("dve", lambda e: e.memset(c.carry[:], 0.0), writes=["carry"])
        for ci in range(c0 + ncs - 1, c0 - 1, -1):
            if brk_ch is not None and ci == brk_ch - 1:
                P.op("dve", lambda e: e.tensor_scalar(out=c.carry[:], in0=c.carry[:], scalar1=c.flag[:, 0:1], scalar2=None,
                                                      op0=ALU.mult), reads=["carry", "flag"], writes=["carry"])
            P.op("dve", lambda e: e.tensor_copy(out=prb, in_=c.carry[:]), reads=["carry"], writes=[K_, "prev"])
            prep(ci, True)
            py, pyk = c.ps[5 + c.pob % 2], ("ps", 5 + c.pob % 2)
            c.pob += 1
            direction(ci, 0, True, py, pyk, True, None, upd=False)
            direction(ci, 1, True, py, pyk, False, None)
            prevs = [prevf[:, (ci - c0) * 512:(ci - c0 + 1) * 512], prb]
            for h in range(8):
                for d in range(2):
                    P.op("pe", lambda e, h=h, d=d, py=py: e.matmul(py[:, h * 64:(h + 1) * 64], lhsT=mb[d][:, h * 128:(h + 1) * 128],
                                                            rhs=xcb[d][:, h * 64:(h + 1) * 64], start=(d == 0), stop=False),
                         reads=[K_], writes=[pyk])
                    P.op("pe", lambda e, h=h, d=d, py=py, prevs=prevs: e.matmul(py[:, h * 64:(h + 1) * 64], lhsT=csb[d][:, h * 128:(h + 1) * 128],
                                                            rhs=prevs[d][:, h * 64:(h + 1) * 64], start=False, stop=(d == 1)),
                         reads=[K_, "prev"], writes=[pyk])
            t0 = ci * 128
            P.op("dve", lambda e: e.tensor_tensor(out=y1, in0=xtm, in1=c.dsk[:], op=ALU.mult), reads=[K_, "dsk"], writes=[K_])
            P.op("dve", lambda e, py=py: e.tensor_tensor(out=y1, in0=py[:, :], in1=y1, op=ALU.add), reads=[pyk, K_], writes=[K_])
            ps, pk = next_ps(c)
            for j in range(4):
                P.op("pe", lambda e, ps=ps, j=j: e.transpose(out=ps[:, j * 128:(j + 1) * 128], in_=y1[:, j * 128:(j + 1) * 128],
                                                             identity=c.identf[:]), reads=[K_, "identf"], writes=[pk])
            P.op("sp", lambda e, t0=t0: e.dma_start(out=zt, in_=zxT[lay["zb"] + 512 * g * gs:lay["zb"] + 512 * g * gs + 512, t0:t0 + 128].rearrange("(k p) t -> p k t", p=128)),
                 reads=[("zxT", lay["zb"] // 128 + 4 * g * gs + j, t0 // 512) for j in range(4)], writes=[K_], slot="ldS")
            P.op("act", lambda e: e.activation(out=zt, in_=zt, func=AF.Silu), reads=[K_], writes=[K_])
            P.op("dve", lambda e, ps=ps: e.tensor_tensor(out=yT, in0=ps[:, :].rearrange("p (k t) -> p k t", t=128), in1=zt, op=ALU.mult),
                 reads=[pk, K_], writes=[K_])
            P.op("act", lambda e: e.activation(out=sqg, in_=yT, func=AF.Square), reads=[K_], writes=[K_])
            pm, pmk = next_ps(c)
            for j in range(4):
                P.op("pe", lambda e, pm=pm, j=j: e.matmul(pm[:, 0:128], lhsT=c.onesg[:], rhs=sqg[:, j, :], start=(j == 0), stop=(j == 3)),
                     reads=[K_, "onesg"], writes=[pmk])
            P.op("dve", lambda e, pm=pm: e.tensor_scalar(out=rs, in0=pm[:, 0:128], scalar1=EPS, scalar2=None, op0=ALU.add),
                 reads=[pmk], writes=[K_])
            P.op("act", lambda e: e.activation(out=rs, in_=rs, func=AF.Sqrt), reads=[K_], writes=[K_])
            P.op("dve", lambda e: e.reciprocal(out=rs, in_=rs), reads=[K_], writes=[K_])
            for j in range(4):
                P.op("dve", lambda e, j=j: e.scalar_tensor_tensor(out=yob[:, j, :], in0=yT[:, j, :], scalar=c.nwg[:, j:j + 1], in1=rs,
                                                                  op0=ALU.mult, op1=ALU.mult), reads=[K_, "nwg"], writes=[K_])
            P.op("sp", lambda e, t0=t0: e.dma_start(
                out=ysT[512 * g * gs:512 * g * gs + 512, t0:t0 + 128].rearrange("(k p) t -> p k t", p=128), in_=yob),
                reads=[K_], writes=[("ysT", 4 * g * gs + j, t0 // 512) for j in range(4)], slot="st_ys")


    for g in range(lay["G"]):
        for (c0, ncs) in seqs:
            group(g, c0, ncs)


def ssd_layer(c, xT, l, li, W, nch, brk_ch):
    load_vec(c, 0, W["mix_norm"][li])
    norm_stage(c, xT, 0, W["hnT"], BF16, "xT", "hnT")

    def epi_zx(f, ci, ps, pk):
        ef, efk, eo, eok = epi_tiles(c)
        c.P.op("act", lambda e: e.activation(out=eo[:], in_=ps[:, :], func=AF.Copy), reads=[pk], writes=[eok])
        c.P.op("sp", lambda e: e.dma_start(out=W["zxT"][f * 128:(f + 1) * 128, ci * 512:(ci + 1) * 512], in_=eo[:]),
               reads=[eok], writes=[("zxT", f, ci)], slot="st_zx")

    gemm_F(c, W["hnT"], "hnT", D, W["ssm_in_w"][l], 0, 10368, epi_zx)
    c.P.op("dve", lambda e: e.memset(c.arA[:, 0:8], 0.0), writes=ARA + ["ssd"])
    ssd_core(c, W["zxT"], W["ysT"], W, l, nch, brk_ch)
    gemm_F(c, W["ysT"], "ysT", DI, W["ssm_out_w"][l], 0, D,
           lambda f, ci, ps, pk: epi_resid(c, xT, None, f, ci, ps, pk))


def build_ssd_only(T, brk_ch):
    nc = bass.Bass("TRN2", target_bir_lowering=False)
    ext = lambda n, s, d=F32: nc.dram_tensor(n, s, d, kind="ExternalInput").ap()
    xin = ext("xT_in", [D, T])
    W = {"mix_norm": ext("mix_norm", [1, D]), "ssm_in_w": ext("ssm_in_w", [1, D, 10368]),
         "ssm_conv_w": ext("ssm_conv_w", [1, 5, 6144]), "ssm_conv_b": ext("ssm_conv_b", [1, 6144]),
         "ssm_dt_bias": ext("ssm_dt_bias", [1, 2, 64]), "ssm_a_log": ext("ssm_a_log", [1, 2, 64]),
         "dskip": ext("dskip", [1, 8, 128, 512]), "ssm_norm_w": ext("ssm_norm_w", [1, DI]),
         "ssm_out_w": ext("ssm_out_w", [1, DI, D])}
    ident_d = ext("ident_in", [128, 128], BF16)
    identf_d = ext("identf_in", [128, 128])
    ucum_d, lcum_d, flag_d = ext("ucum_in", [128, 128]), ext("lcum_in", [128, 128]), ext("flag_in", [128, 1])
    yT = nc.dram_tensor("yT", [D, T], F32, kind="ExternalOutput").ap()
    W["hnT"] = nc.dram_tensor("hnT", [D, T], BF16).ap()
    W["zxT"] = RowSplit(nc.dram_tensor("zxA", [4096, T], F32).ap(), nc.dram_tensor("zxB", [6272, T], F32).ap(), 4096)
    W["ysT"] = nc.dram_tensor("ysT", [DI, T], BF16).ap()
    with ExitStack() as es:
        c = setup(nc, es, T)
        na_setup(c, es)
        ssd_setup(c, es)
        P = c.P
        load_ident(c, ident_d, identf_d)
        ssd_consts(c, ucum_d, lcum_d, flag_d)
        for ci in range(T // 512):
            for f in range(16):
                P.op("sp", lambda e, f=f, ci=ci: e.dma_start(
                    out=yT[f * 128:(f + 1) * 128, ci * 512:(ci + 1) * 512],
                    in_=xin[f * 128:(f + 1) * 128, ci * 512:(ci + 1) * 512]),
                    writes=[("xT", f, ci)], slot="st_xT")
        ssd_layer(c, yT, 0, 0, W, T // 128, brk_ch)
        P.finish(["st_xT"])
        P.emit()
    return nc


def dskip_table(ssm_d_l):
    t = np.repeat(ssm_d_l.reshape(8, 8), 64, axis=1)
    return np.ascontiguousarray(np.broadcast_to(t[:, None, :], (8, 128, 512))).astype(np.float32)


NCORES, TL, TA = 8, 2048, 16384
SEQ_TOK = [(0, 4096), (4096, 4096), (8192, 8192)]
NA_SEQS = [(t0, n // 64) for t0, n in SEQ_TOK]
SSD_SEQS = [(t0 // 128, n // 128) for t0, n in SEQ_TOK]
W8 = [("mix_norm", [4, D]), ("na_out_w", [2, D, D]), ("na_out_b", [2, D]), ("ssm_out_w", [2, DI, D]),
      ("mlp_norm", [4, D]), ("mlp_up_w", [4, D, HID]), ("mlp_down_w", [4, HID, D]), ("final_norm", [D]),
      ("na_qkv_w_s", [2, D, 768]), ("na_qkv_b_s", [2, 768]), ("nabias", [2, 2, 64, 2, 15, 64]),
      ("ssm_in_w_s", [2, D, 1408]), ("ssm_conv_w", [2, 5, 768]), ("ssm_conv_b", [2, 768]),
      ("ssm_dt_bias", [2, 2, 8]), ("ssm_a_log", [2, 2, 8]), ("dskip", [2, 128, 512]), ("ssm_norm_w", [2, 512])]


def allgather(c, src, dst, rkeys, wkeys):
    c.P.op("pool", lambda e: e.collective_compute("AllGather", ALU.bypass, replica_groups=[list(range(NCORES))],
                                                  ins=[src], outs=[dst]),
           reads=rkeys, writes=wkeys, embed=False)


def own_cols(ap):
    return lambda e: ap[:, bass.ds(e.partition_id() * TL, TL)]


def mixer_in(c, xT, li, W, pp):
    load_vec(c, 0, W["mix_norm"][li])
    norm_stage(c, xT, 0, W["hn_loc"], BF16, "xT", "hnL")
    allgather(c, W["hn_loc"], W["hn_all"][pp], [("hnL", k, ci) for k in range(16) for ci in range(TL // 512)], [("hnA", pp)])
    return [(lambda e, r=r: W["hn_all"][pp][r * D:(r + 1) * D, :], [("hnA", pp)], r * (TL // 512)) for r in range(NCORES)]


def na_layer8(c, xT, l, li, W, pp):
    groups = mixer_in(c, xT, li, W, pp)
    load_vec(c, 1, W["na_qkv_b_s"][l])

    def epi_qkv(f, ci, ps, pk):
        ef, efk, eo, eok = epi_tiles(c)
        ob = eo[:, 0:256].bitcast(BF16)
        b = c.vecs[:, 16 + f:16 + f + 1]
        c.P.op("dve", lambda e: e.tensor_scalar(out=ob, in0=ps[:, :], scalar1=b, scalar2=None, op0=ALU.add),
               reads=[pk, ("vec", 1)], writes=[eok])
        c.P.op("sp", lambda e: e.dma_start(out=W["qkv_m"][f * 128:(f + 1) * 128, ci * 512:(ci + 1) * 512], in_=ob),
               reads=[eok], writes=[("qkvT", f, ci)], slot="st_qkv")

    gemm_F(c, None, None, D, W["na_qkv_w_s"][l], 0, 768, epi_qkv, groups=groups)
    na_core(c, W["qkv_m"], W["o_m"], W["nabias"][l], None, None, seqs=NA_SEQS, npairs=2, qstride=256)
    allgather(c, W["o_m"], W["o_all"][pp], [("oT", p, ci) for p in range(2) for ci in range(TA // 512)], [("oA", pp)])
    load_vec(c, 4, W["na_out_b"][l])
    gemm_F(c, None, None, D, W["na_out_w"][l], 0, D, lambda f, ci, ps, pk: epi_resid(c, xT, 4, f, ci, ps, pk),
           groups=[(own_cols(W["o_all"][pp]), [("oA", pp)], 0)])


def ssd_layer8(c, xT, l, li, W, pp):
    groups = mixer_in(c, xT, li, W, pp)

    def epi_zx(f, ci, ps, pk):
        ef, efk, eo, eok = epi_tiles(c)
        c.P.op("act", lambda e: e.activation(out=eo[:], in_=ps[:, :], func=AF.Copy), reads=[pk], writes=[eok])
        c.P.op("sp", lambda e: e.dma_start(out=W["zx_m"][f * 128:(f + 1) * 128, ci * 512:(ci + 1) * 512], in_=eo[:]),
               reads=[eok], writes=[("zxT", f, ci)], slot="st_zx")

    gemm_F(c, None, None, D, W["ssm_in_w_s"][l], 0, 1408, epi_zx, groups=groups)
    c.P.op("dve", lambda e: e.memset(c.arA[:, 0:8], 0.0), writes=ARA + ["ssd"])
    lay = dict(G=1, zb=0, xb=512, bb=1024, cb=1152, dtb=1280, dtd=8, gs=0, ccB=512, ccC=640, dsk=lambda g: W["dskip"][l])
    ssd_core(c, W["zx_m"], W["ys_m"], W, l, None, None, lay=lay, seqs=SSD_SEQS)
    allgather(c, W["ys_m"], W["ys_all"][pp], [("ysT", j, ci) for j in range(4) for ci in range(TA // 512)], [("yA", pp)])
    gemm_F(c, None, None, DI, W["ssm_out_w"][l], 0, D, lambda f, ci, ps, pk: epi_resid(c, xT, None, f, ci, ps, pk),
           groups=[(own_cols(W["ys_all"][pp]), [("yA", pp)], 0)])


def build_full8():
    nc = bass.Bass("TRN2", target_bir_lowering=False, num_devices=NCORES)
    ext = lambda n, s, d=F32: nc.dram_tensor(n, s, d, kind="ExternalInput").ap()
    xin = ext("xT_in", [D, TL])
    W = {n: ext(n, s) for n, s in W8}
    ident_d = ext("ident_in", [128, 128], BF16)
    identf_d = ext("identf_in", [128, 128])
    ucum_d, lcum_d, flag_d = ext("ucum_in", [128, 128]), ext("lcum_in", [128, 128]), ext("flag_in", [128, 1])
    yT = nc.dram_tensor("yT", [D, TL], F32, kind="ExternalOutput").ap()
    dr = lambda n, s, d: nc.dram_tensor(n, s, d).ap()
    xT = dr("xres", [D, TL], F32)
    W["hn_loc"] = dr("hn_loc", [D, TL], BF16)
    W["hnT"] = dr("hnT", [D, TL], BF16)
    W["hn_all"] = [dr("hn_all%d" % i, [NCORES * D, TL], BF16) for i in range(2)]
    W["qkv_m"] = dr("qkv_m", [768, TA], BF16)
    W["o_m"] = dr("o_m", [256, TA], BF16)
    W["o_all"] = [dr("o_all%d" % i, [D, TA], BF16) for i in range(2)]
    W["zx_m"] = dr("zx_m", [1408, TA], F32)
    W["ys_m"] = dr("ys_m", [512, TA], BF16)
    W["ys_all"] = [dr("ys_all%d" % i, [DI, TA], BF16) for i in range(2)]
    hT = dr("hT", [HID, TL], BF16)
    with ExitStack() as es:
        c = setup(nc, es, TL)
        na_setup(c, es)
        ssd_setup(c, es)
        P = c.P
        load_ident(c, ident_d, identf_d)
        ssd_consts(c, ucum_d, lcum_d, flag_d)
        P.op("dve", lambda e: e.memset(c.pen[:], 0.0), writes=["pen"])
        for ci in range(TL // 512):
            for f in range(16):
                P.op("sp", lambda e, f=f, ci=ci: e.dma_start(
                    out=xT[f * 128:(f + 1) * 128, ci * 512:(ci + 1) * 512],
                    in_=xin[f * 128:(f + 1) * 128, ci * 512:(ci + 1) * 512]),
                    writes=[("xT", f, ci)], slot="st_xT")
        for i in range(4):
            if i % 2 == 0:
                na_layer8(c, xT, i // 2, i, W, (i // 2) % 2)
            else:
                ssd_layer8(c, xT, i // 2, i, W, (i // 2) % 2)
            load_vec(c, 0, W["mlp_norm"][i])
            norm_stage(c, xT, 0, W["hnT"], BF16, "xT", "hnT")
            mlp_stage(c, W["hnT"], W["mlp_up_w"][i], hT, W["mlp_down_w"][i], xT)
        load_vec(c, 0, W["final_norm"])
        norm_stage(c, xT, 0, yT, F32, "xT", "yT")
        P.finish(["st_yT"])
        P.emit()
    return nc


def shard_inputs(inp):
    import ml_dtypes
    f32 = lambda a: np.ascontiguousarray(np.asarray(a, dtype=np.float32))
    x_all = np.concatenate([f32(inp["x_prompt"]).reshape(-1, D), f32(inp["x_sample"]).reshape(-1, D)], 0)
    full = {n: f32(inp[n]) for n in ("mix_norm", "na_out_w", "na_out_b", "ssm_out_w", "mlp_norm", "mlp_up_w",
                                      "mlp_down_w", "final_norm")}
    qw, qb, rpb = f32(inp["na_qkv_w"]), f32(inp["na_qkv_b"]), f32(inp["na_rpb"])
    iw, cw, cb = f32(inp["ssm_in_w"]), f32(inp["ssm_conv_w"]), f32(inp["ssm_conv_b"])
    dtb, alog, dsk, nw = f32(inp["ssm_dt_bias"]), f32(inp["ssm_a_log"]), f32(inp["ssm_d"]), f32(inp["ssm_norm_w"])
    nab = [na_bias_table(rpb[l]) for l in range(2)]
    dst = [dskip_table(dsk[l]) for l in range(2)]
    tri = np.triu(np.ones((128, 128), np.float32))
    const = dict(ident_in=np.eye(128).astype(ml_dtypes.bfloat16), identf_in=np.eye(128, dtype=np.float32),
                 ucum_in=tri, lcum_in=np.ascontiguousarray(tri.T), flag_in=np.ones((128, 1), np.float32))
    maps = []
    for c in range(NCORES):
        m = dict(full)
        m.update(const)
        m["xT_in"] = np.ascontiguousarray(x_all[c * TL:(c + 1) * TL].T)
        qs = [slice(i * D + 256 * c, i * D + 256 * c + 256) for i in range(3)]
        m["na_qkv_w_s"] = np.ascontiguousarray(np.concatenate([qw[:, :, q] for q in qs], 2))
        m["na_qkv_b_s"] = np.ascontiguousarray(np.concatenate([qb[:, q] for q in qs], 1))
        m["nabias"] = np.ascontiguousarray(np.stack([nab[l][2 * c:2 * c + 2] for l in range(2)]))
        cols = [slice(512 * c, 512 * c + 512), slice(4096 + 512 * c, 4096 + 512 * c + 512),
                slice(8192 + 128 * c, 8192 + 128 * c + 128), slice(9216 + 128 * c, 9216 + 128 * c + 128),
                slice(10240 + 8 * c, 10240 + 8 * c + 8), slice(10304 + 8 * c, 10304 + 8 * c + 8)]
        m["ssm_in_w_s"] = np.ascontiguousarray(np.concatenate([iw[:, :, q] for q in cols] + [np.zeros((2, D, 112), np.float32)], 2))
        cc = [slice(512 * c, 512 * c + 512), slice(4096 + 128 * c, 4096 + 128 * c + 128), slice(5120 + 128 * c, 5120 + 128 * c + 128)]
        m["ssm_conv_w"] = np.ascontiguousarray(np.concatenate([cw[:, :, q] for q in cc], 2))
        m["ssm_conv_b"] = np.ascontiguousarray(np.concatenate([cb[:, q] for q in cc], 1))
        m["ssm_dt_bias"] = np.ascontiguousarray(dtb[:, :, 8 * c:8 * c + 8])
        m["ssm_a_log"] = np.ascontiguousarray(alog[:, :, 8 * c:8 * c + 8])
        m["dskip"] = np.ascontiguousarray(np.stack([dst[l][c] for l in range(2)]))
        m["ssm_norm_w"] = np.ascontiguousarray(nw[:, 512 * c:512 * c + 512])
        maps.append(m)
    return maps


def kernel(**inp):
    maps = shard_inputs(inp)
    res = run_bass_kernel_spmd(build_full8(), maps, core_ids=list(range(NCORES)))
    y_all = np.concatenate([np.asarray(res.results[c]["yT"], dtype=np.float32).T for c in range(NCORES)], 0)
    return (np.ascontiguousarray(y_all[:8192]).reshape(2, 4096, D), np.ascontiguousarray(y_all[8192:]).reshape(1, 8192, D))
```
